# Optimizing a Trainium2 kernel written in Bass

```python
import jax, jax.numpy as jnp
from jax import lax
import numpy as np

D_MODEL = 1024
BATCH = 2
SEQ = 8192
DEPTH = 2
DEC_BATCH = 8
DEC_SEQ = 32
PAST_LEN = 4096

CHUNK = 64
D_CONV = D_MODEL
CONV_WIDTH = 31
RET_HEADS = 4
RET_DK = D_MODEL // RET_HEADS
RET_DV = 2 * RET_DK
RET_QK = RET_HEADS * RET_DK
RET_V = RET_HEADS * RET_DV
D_FF = 4 * D_MODEL
ROPE_BASE = 10000.0
RMS_EPS = 1e-6
LN_EPS = 1e-5

OFF_Q = 2 * D_CONV
OFF_K = OFF_Q + RET_QK
OFF_V = OFF_K + RET_QK
OFF_G = OFF_V + RET_V
OFF_GC = OFF_G + RET_V
OFF_GR = OFF_GC + D_MODEL
D_IN = OFF_GR + D_MODEL

kernel_name = "gated_conformer_retention_streaming_step"


def rmsnorm(x, g):
    xf = x.astype(jnp.float32)
    y = xf * lax.rsqrt(jnp.mean(xf * xf, axis=-1, keepdims=True) + RMS_EPS)
    return (y * g.astype(jnp.float32)).astype(x.dtype)


def layernorm(x, g, b):
    xf = x.astype(jnp.float32)
    mu = jnp.mean(xf, axis=-1, keepdims=True)
    var = jnp.mean(jnp.square(xf - mu), axis=-1, keepdims=True)
    y = (xf - mu) * lax.rsqrt(var + LN_EPS)
    return (y * g.astype(jnp.float32) + b.astype(jnp.float32)).astype(x.dtype)


def rope(x, pos):
    dk = x.shape[-1]
    inv_freq = ROPE_BASE ** (-jnp.arange(0, dk, 2, dtype=jnp.float32) / dk)
    ang = pos[:, None] * inv_freq[None, :]
    cos = jnp.cos(ang)[None, :, None, :].astype(x.dtype)
    sin = jnp.sin(ang)[None, :, None, :].astype(x.dtype)
    x1, x2 = x[..., : dk // 2], x[..., dk // 2:]
    return jnp.concatenate([x1 * cos - x2 * sin, x2 * cos + x1 * sin], axis=-1)


def retention_log_decay():
    return jnp.log(1.0 - jnp.exp2(-5.0 - jnp.arange(RET_HEADS, dtype=jnp.float32)))


def retention_seq(q, k, v, S0):
    B, H, L, DK = q.shape
    DV = v.shape[-1]
    c = min(CHUNK, L)
    n_blk = L // c
    lg = retention_log_decay()[:, None, None]
    idx = jnp.arange(c, dtype=jnp.float32)
    diff = idx[:, None] - idx[None, :]
    dmask = jnp.where(diff >= 0, jnp.exp(jnp.maximum(diff, 0.0)[None] * lg), 0.0)
    q_dec = jnp.exp((idx + 1.0)[None, :, None] * lg)
    k_dec = jnp.exp((c - 1.0 - idx)[None, :, None] * lg)
    s_dec = jnp.exp(c * lg)

    def blk(S, qkv):
        qb, kb, vb = qkv
        att = jnp.einsum('bhnd,bhmd->bhnm', qb, kb) * dmask
        o = (jnp.einsum('bhnm,bhme->bhne', att, vb)
             + jnp.einsum('bhnd,bhde->bhne', qb, S) * q_dec)
        S = s_dec * S + jnp.einsum('bhmd,bhme->bhde', kb * k_dec, vb)
        return S, o

    def split(t):
        return jnp.moveaxis(t.astype(jnp.float32).reshape(B, H, n_blk, c, t.shape[-1]), 2, 0)

    S, o = lax.scan(blk, S0.astype(jnp.float32), (split(q), split(k), split(v)))
    o = jnp.moveaxis(o, 0, 2).reshape(B, H, L, DV)
    return o, S


def mixer(h, pos0, conv_state, ret_state, w_in, conv_w, conv_b, conv_ln_g, conv_ln_b,
          w_conv_out, ret_gn_g, w_ret_out, w_out):
    B, L, _ = h.shape
    P = h @ w_in

    a, b = P[..., :D_CONV], P[..., D_CONV:OFF_Q]
    glu = a * jax.nn.sigmoid(b)
    full = jnp.concatenate([conv_state.astype(glu.dtype), glu], axis=1)
    yc = lax.conv_general_dilated(full, conv_w[:, None, :].astype(full.dtype),
                                  window_strides=(1,), padding='VALID',
                                  dimension_numbers=('NWC', 'WIO', 'NWC'),
                                  feature_group_count=D_CONV) + conv_b
    new_conv_state = full[:, -(CONV_WIDTH - 1):, :]
    yc = jax.nn.silu(layernorm(yc, conv_ln_g, conv_ln_b))
    conv_out = yc @ w_conv_out

    pos = pos0 + jnp.arange(L, dtype=jnp.float32)
    q = rope(P[..., OFF_Q:OFF_K].reshape(B, L, RET_HEADS, RET_DK), pos)
    k = rope(P[..., OFF_K:OFF_V].reshape(B, L, RET_HEADS, RET_DK), pos) * (RET_DK ** -0.5)
    v = P[..., OFF_V:OFF_G].reshape(B, L, RET_HEADS, RET_DV)
    o, new_ret_state = retention_seq(q.transpose(0, 2, 1, 3), k.transpose(0, 2, 1, 3),
                                     v.transpose(0, 2, 1, 3), ret_state)
    o = o.transpose(0, 2, 1, 3)
    mu = jnp.mean(o, axis=-1, keepdims=True)
    var = jnp.mean(jnp.square(o - mu), axis=-1, keepdims=True)
    o = ((o - mu) * lax.rsqrt(var + LN_EPS)).reshape(B, L, RET_V)
    o = (o * ret_gn_g.astype(jnp.float32)).astype(h.dtype)
    ret_out = (jax.nn.silu(P[..., OFF_G:OFF_GC]) * o) @ w_ret_out

    g_conv = jax.nn.sigmoid(P[..., OFF_GC:OFF_GR])
    g_ret = jax.nn.sigmoid(P[..., OFF_GR:D_IN])
    out = (g_conv * conv_out + g_ret * ret_out) @ w_out
    return out, new_conv_state, new_ret_state


def trunk(x, pos0, conv_states, ret_states, norm1_g, w_in, conv_w, conv_b, conv_ln_g,
          conv_ln_b, w_conv_out, ret_gn_g, w_ret_out, w_out, norm2_g, w_mlp1, w_mlp2, final_g):
    new_conv, new_ret = [], []
    for l in range(DEPTH):
        m, cs, rs = mixer(rmsnorm(x, norm1_g[l]), pos0, conv_states[l], ret_states[l],
                          w_in[l], conv_w[l], conv_b[l], conv_ln_g[l], conv_ln_b[l],
                          w_conv_out[l], ret_gn_g[l], w_ret_out[l], w_out[l])
        x = x + m
        hm = rmsnorm(x, norm2_g[l]) @ w_mlp1[l]
        x = x + jnp.square(jax.nn.relu(hm)) @ w_mlp2[l]
        new_conv.append(cs)
        new_ret.append(rs.astype(x.dtype))
    return rmsnorm(x, final_g), jnp.stack(new_conv), jnp.stack(new_ret)


def setup_inputs(seed: int = 0) -> dict:
    key = jax.random.key(seed)
    ks = jax.random.split(key, 20)
    f32 = jnp.float32
    nrm = lambda k, s: jax.random.normal(k, s, f32)
    return {
        "x_prompt": nrm(ks[0], (BATCH, SEQ, D_MODEL)),
        "x_sample": nrm(ks[1], (DEC_BATCH, DEC_SEQ, D_MODEL)),
        "state_conv": 0.5 * nrm(ks[2], (DEPTH, DEC_BATCH, CONV_WIDTH - 1, D_CONV)),
        "state_ret": 0.1 * nrm(ks[3], (DEPTH, DEC_BATCH, RET_HEADS, RET_DK, RET_DV)),
        "norm1_g": 1.0 + 0.01 * nrm(ks[4], (DEPTH, D_MODEL)),
        "w_in": nrm(ks[5], (DEPTH, D_MODEL, D_IN)) * D_MODEL ** -0.5,
        "conv_w": nrm(ks[6], (DEPTH, CONV_WIDTH, D_CONV)) * CONV_WIDTH ** -0.5,
        "conv_b": 0.01 * nrm(ks[7], (DEPTH, D_CONV)),
        "conv_ln_g": 1.0 + 0.01 * nrm(ks[8], (DEPTH, D_CONV)),
        "conv_ln_b": 0.01 * nrm(ks[9], (DEPTH, D_CONV)),
        "w_conv_out": nrm(ks[10], (DEPTH, D_CONV, D_MODEL)) * D_CONV ** -0.5,
        "ret_gn_g": 1.0 + 0.01 * nrm(ks[11], (DEPTH, RET_V)),
        "w_ret_out": nrm(ks[12], (DEPTH, RET_V, D_MODEL)) * RET_V ** -0.5,
        "w_out": nrm(ks[13], (DEPTH, D_MODEL, D_MODEL)) * D_MODEL ** -0.5,
        "norm2_g": 1.0 + 0.01 * nrm(ks[14], (DEPTH, D_MODEL)),
        "w_mlp1": nrm(ks[15], (DEPTH, D_MODEL, D_FF)) * D_MODEL ** -0.5,
        "w_mlp2": nrm(ks[16], (DEPTH, D_FF, D_MODEL)) * D_FF ** -0.5,
        "final_g": 1.0 + 0.01 * nrm(ks[17], (D_MODEL,)),
    }


def reference(x_prompt, x_sample, state_conv, state_ret, norm1_g, w_in, conv_w, conv_b,
              conv_ln_g, conv_ln_b, w_conv_out, ret_gn_g, w_ret_out, w_out, norm2_g,
              w_mlp1, w_mlp2, final_g):
    weights = (norm1_g, w_in, conv_w, conv_b, conv_ln_g, conv_ln_b, w_conv_out,
               ret_gn_g, w_ret_out, w_out, norm2_g, w_mlp1, w_mlp2, final_g)
    zc = jnp.zeros((DEPTH, x_prompt.shape[0], CONV_WIDTH - 1, D_CONV), x_prompt.dtype)
    zr = jnp.zeros((DEPTH, x_prompt.shape[0], RET_HEADS, RET_DK, RET_DV), jnp.float32)
    y_prompt, conv_p, ret_p = trunk(x_prompt, 0.0, zc, zr, *weights)
    y_sample, conv_s, ret_s = trunk(x_sample, float(PAST_LEN), state_conv, state_ret, *weights)
    return (y_prompt, y_sample, conv_p, ret_p, conv_s, ret_s)
```

```python
import os
import numpy as np
import concourse.bass as bass
import concourse.mybir as mybir
from concourse.bass_utils import run_bass_kernel_spmd

F32 = mybir.dt.float32
BF16 = mybir.dt.bfloat16
ALU = mybir.AluOpType
AF = mybir.ActivationFunctionType

NCORES = 8
D = 1024
SEQ = 8192
SEG = 8192
TW = 512
NPT = SEG // TW
SW = 32
NTOK = SEG + SW
PAST = 4096
HEADS = 4
DIN = 10240
OFF_Q, OFF_K, OFF_V, OFF_G, OFF_GC, OFF_GR = 2048, 3072, 4096, 6144, 8192, 9216
RMS_EPS = 1e-6
LN_EPS = 1e-5
NB = 4
NTMP = 6
GAMMA = [1.0 - 2.0 ** (-5 - h) for h in range(HEADS)]
VL = 304
V_G1, V_G2, V_CB, V_LNG, V_LNB, V_GNG, V_CW = 0, 8, 16, 24, 32, 40, 56
V_FG = 2 * VL
NV = 2 * VL + 8
C_DQ, C_DK, C_MASK, C_ID = 0, 512, 1024, 1152
T_COS, T_SIN, T_C = 0, NTOK, 2 * NTOK
NTAB = 2 * NTOK + 1280


class V:
    __slots__ = ("ap", "key", "lo", "hi")

    def __init__(self, ap, key, lo, hi):
        self.ap, self.key, self.lo, self.hi = ap, key, lo, hi

    def p(self, n):
        return V(self.ap[0:n], self.key, self.lo, self.hi)


class Buf:
    def __init__(self, key, ap, shape, esize, base=0):
        self.key, self.shape, self.esize, self.base = key, tuple(shape), esize, base
        n = int(np.prod(shape))
        self.flat = ap
        if len(shape) == 2:
            ap = ap.rearrange("p (a b) -> p a b", a=shape[0])
        elif len(shape) == 3:
            ap = ap.rearrange("p (a b c) -> p a b c", a=shape[0], b=shape[1])
        self.ap = ap
        st, acc = [], 1
        for s in reversed(shape):
            st.append(acc)
            acc *= s
        self.strides = list(reversed(st))
        self.n = n

    def __getitem__(self, idx):
        if not isinstance(idx, tuple):
            idx = (idx,)
        idx = list(idx) + [slice(None)] * (len(self.shape) - len(idx))
        lo = hi = 0
        for i, s, st in zip(idx, self.shape, self.strides):
            if isinstance(i, slice):
                a = 0 if i.start is None else i.start
                b = s if i.stop is None else i.stop
            else:
                a, b = i, i + 1
            assert 0 <= a < b <= s, (self.key, idx, self.shape)
            lo += a * st
            hi += (b - 1) * st
        ap = self.ap[(slice(None),) + tuple(idx)]
        return V(ap, self.key, self.base + lo * self.esize, self.base + (hi + 1) * self.esize)


class Prog:
    def __init__(self, dry):
        self.dry = dry
        self.ops = []
        self.acc = {}
        self.dcount = {}
        self.epoch = 0

    def add(self, eng, emit, reads=(), writes=(), kind="c", sem=None, n=1):
        if self.dry:
            return
        deps = set()
        oid = len(self.ops)
        for v in reads:
            for r in self.acc.get(v.key, ()):
                if r[3] and r[0] < v.hi and v.lo < r[1]:
                    deps.add(r[2])
        for v in writes:
            for r in self.acc.get(v.key, ()):
                if r[0] < v.hi and v.lo < r[1]:
                    deps.add(r[2])
        for v in writes:
            lst = [r for r in self.acc.get(v.key, []) if not (v.lo <= r[0] and r[1] <= v.hi)]
            lst.append((v.lo, v.hi, oid, True))
            self.acc[v.key] = lst
        for v in reads:
            self.acc.setdefault(v.key, []).append((v.lo, v.hi, oid, False))
        deps.discard(oid)
        cum = None
        if kind != "c":
            cum = self.dcount.get(sem, 0) + (16 * n if kind == "d" else 1)
            self.dcount[sem] = cum
        self.ops.append(dict(eng=eng, emit=emit, deps=deps, kind=kind, sem=sem, cum=cum, ep=self.epoch))


class WStream:
    def __init__(self, P, slots, seq):
        self.P, self.slots = P, slots
        self.rec = [] if seq is None else None
        self.seq = seq
        self.i = 0
        self.issued = 0
        self.open = {}
        self.finished = set()
        if seq is not None:
            for _ in range(min(NB, len(seq))):
                self._issue()

    def _issue(self):
        i = self.issued
        key, pieces, nk, nc_ = self.seq[i]
        slot = self.slots[i % NB]
        wr = slot[0:nk * nc_]
        view = wr.ap.rearrange("p (k n) -> p k n", k=nk)

        def emit(e, pieces=pieces, view=view):
            return [e.dma_start(out=view[:, :, c0:c0 + src.shape[2]], in_=src) for (src, c0) in pieces]
        self.P.add("pool", emit, reads=[], writes=[slot[:]], kind="d", sem=("w", i % NB), n=len(pieces))
        self.issued += 1

    def get(self, key, pieces, nk, nc_):
        i = self.i
        self.i += 1
        if self.rec is not None:
            self.rec.append((key, pieces, nk, nc_))
        else:
            assert self.seq[i][0] == key, (self.seq[i][0], key)
            assert i < self.issued, ("weight unit requested before its slot was free", key, sorted(self.open))
        slot = self.slots[i % NB]
        self.open[key] = i

        def wv(kc, c0, c1, slot=slot, nc_=nc_):
            return slot[kc * nc_ + c0: kc * nc_ + c1]
        return wv

    def done(self, key):
        i = self.open.pop(key)
        self.finished.add(i)
        if self.seq is not None:
            while self.issued < len(self.seq) and (self.issued - NB) in self.finished:
                self._issue()


class Tile_:
    def __init__(self, off, W, chunks, sample):
        self.off, self.W, self.chunks, self.sample = off, W, chunks, sample
        self.gC = [g ** chunks[0][1] for g in GAMMA]


def build_program(nc, dry, seq, cache):
    P = Prog(dry)

    def memo(kind, name, fn):
        k = (kind, name)
        if k not in cache:
            cache[k] = fn()
        return cache[k]

    def dt(name, shape, dtype, **kw):
        return memo("d", name, lambda: nc.dram_tensor(name, shape, dtype, **kw))
    xin = dt("xin", [D, NTOK], F32, kind="ExternalInput").ap()
    stc = dt("stc", [2, D, 30], F32, kind="ExternalInput").ap()
    strt = dt("strt", [2, HEADS, 256, 512], F32, kind="ExternalInput").ap()
    w_in = dt("w_in", [2, D, DIN], F32, kind="ExternalInput").ap()
    w_co = dt("w_co", [2, D, D], F32, kind="ExternalInput").ap()
    w_ro = dt("w_ro", [2, 2 * D, D], F32, kind="ExternalInput").ap()
    w_o = dt("w_o", [2, D, D], F32, kind="ExternalInput").ap()
    w_m1 = dt("w_m1", [2, D, 4 * D], F32, kind="ExternalInput").ap()
    w_m2 = dt("w_m2", [2, 4 * D, D], F32, kind="ExternalInput").ap()
    vecs_d = dt("vecs", [128, NV], F32, kind="ExternalInput").ap()
    tabs_d = dt("tabs", [128, NTAB], F32, kind="ExternalInput").ap()
    coef_d = dt("coef", [128, 20], F32, kind="ExternalInput").ap()
    y_d = dt("y", [D, NTOK], F32, kind="ExternalOutput").ap()
    ncs_p = dt("ncs_p", [2, D, 30], F32, kind="ExternalOutput").ap()
    nsr_p = dt("nsr_p", [2, HEADS, 256, 512], F32, kind="ExternalOutput").ap()
    ncs_s = dt("ncs_s", [2, D, 30], F32, kind="ExternalOutput").ap()
    nsr_s = dt("nsr_s", [2, HEADS, 256, 512], F32, kind="ExternalOutput").ap()
    x1_d = dt("x1s", [D, NTOK], F32).ap()
    loc_d = [dt(f"loc{l}", [128, 4352], F32).ap() for l in range(2)]
    gath_d = [dt(f"gath{l}", [4 * 128, 4352], F32).ap() for l in range(2)]

    def fm(ap):
        return ap.rearrange("(c p) t -> p c t", p=128)

    def sb(name, shape, dtype):
        es = 4 if dtype == F32 else 2
        n = int(np.prod(shape))
        t = memo("s", name, lambda: nc.alloc_sbuf_tensor("sb_" + name, [128, n], dtype)).ap()
        return Buf(name, t, shape, es)

    xt = sb("xt", [8, TW], F32)
    hT = sb("hT", [8, TW], BF16)
    sqb = sb("sqb", [8, TW], BF16)
    tmp = sb("tmp", [NTMP, TW], F32)
    dslots = Buf("sqb", sqb.flat, [32, 128], 2)
    arena_ap = memo("s", "arena", lambda: nc.alloc_sbuf_tensor("arena", [128, 8448], F32)).ap()
    glu = Buf("arena", arena_ap[:, 0:4 * 544].bitcast(BF16), [8, 544], 2, 0)
    accb = Buf("arena", arena_ap[:, 4352:4352 + 4096], [8, TW], 4, 4352 * 4)
    ubuf = Buf("arena", arena_ap[:, 0:8192].bitcast(BF16), [32, TW], 2, 0)
    retin = sb("retin", [16, TW], BF16)
    tmpS = Buf("arena", arena_ap[:, 0:4352], [4352], 4, 0)
    xo = Buf("retin", retin.flat.bitcast(F32), [8, TW], 4)
    ycm = sb("ycm", [8, TW], BF16)
    mA = sb("mA", [8, TW], BF16)
    qk_s = [sb(f"qk{i}", [4, TW], BF16) for i in range(2)]
    kpp_s = [sb(f"kpp{i}", [4, 256], BF16) for i in range(2)]
    vt_s = [sb(f"vt{i}", [4, 512], BF16) for i in range(2)]
    gate = sb("gate", [4, TW], BF16)
    attT = sb("attT", [128], BF16)
    on = sb("on", [4, 512], BF16)
    S = sb("S", [8, 512], F32)
    Sbf = sb("Sbf", [8, 512], BF16)
    halo = sb("halo", [8, 30], F32)
    halo_n = sb("halo_n", [8, 30], F32)
    cs = sb("cs", [2, TW], F32)
    ctab = sb("ctab", [1280], F32)
    identb = sb("identb", [128], BF16)
    onesb = sb("onesb", [128], BF16)
    vecs = sb("vecs", [NV], F32)
    st6 = sb("st6", [6], F32)
    mv = sb("mv", [2], F32)
    rs = sb("rs", [1], F32)
    nmr = sb("nmr", [1], F32)
    epsR = sb("epsR", [1], F32)
    epsL = sb("epsL", [1], F32)
    wslots = [sb(f"wb{i}", [4096], BF16) for i in range(NB)]

    psb = [Buf(f"ps{i}", memo("p", f"ps{i}", lambda i=i: nc.alloc_psum_tensor(f"ps{i}", [128, 512], F32)).ap(), [512], 4) for i in range(7)]
    ps7 = Buf("ps7", memo("p", "ps7", lambda: nc.alloc_psum_tensor("ps7", [128, 1024], BF16)).ap(), [1024], 2)
    mmi = [0]

    def mm():
        b = psb[mmi[0] % 2]
        mmi[0] += 1
        return b
    ps_att, ps_o2, ps_kv = psb[4], [psb[3], psb[5]], psb[6]
    ps_conv = psb[2]

    WS = WStream(P, wslots, seq)

    def pe_group(out, pairs):
        n = len(pairs)

        def emit(e, out=out, pairs=pairs, n=n):
            ins = None
            for i, (l, r) in enumerate(pairs):
                ins = e.matmul(out.ap, l.ap, r.ap, start=(i == 0), stop=(i == n - 1))
            return ins
        P.add("pe", emit, reads=[v for pr in pairs for v in pr], writes=[out])

    def pe_acc(out, pairs, start, stop):
        n = len(pairs)

        def emit(e):
            ins = None
            for i, (l, r) in enumerate(pairs):
                ins = e.matmul(out.ap, l.ap, r.ap, start=(start and i == 0), stop=(stop and i == n - 1))
            return ins
        P.add("pe", emit, reads=[v for pr in pairs for v in pr], writes=[out])

    def pe_transposes(items):
        def emit(e):
            ins = None
            for (o, i_, idn) in items:
                ins = e.transpose(o.ap, i_.ap, idn.ap)
            return ins
        P.add("pe", emit, reads=[x for it in items for x in (it[1], it[2])], writes=[it[0] for it in items])

    def act(out, in_, func, scale=1.0, bias=None, extra_reads=()):
        def emit(e):
            kw = {}
            if bias is not None:
                kw["bias"] = bias.ap if isinstance(bias, V) else bias
            sc = scale.ap if isinstance(scale, V) else scale
            return e.activation(out=out.ap, in_=in_.ap, func=func, scale=sc, **kw)
        rd = [in_] + [x for x in (scale, bias) if isinstance(x, V)] + list(extra_reads)
        P.add("act", emit, reads=rd, writes=[out])

    def tt(out, a, b, op):
        P.add("dve", lambda e: e.tensor_tensor(out=out.ap, in0=a.ap, in1=b.ap, op=op), reads=[a, b], writes=[out])

    def ts(out, a, s1, s2, op0, op1=None, eng="dve"):
        def emit(e):
            x1 = s1.ap if isinstance(s1, V) else s1
            x2 = s2.ap if isinstance(s2, V) else s2
            if op1 is None:
                return e.tensor_scalar(out=out.ap, in0=a.ap, scalar1=x1, scalar2=None, op0=op0)
            return e.tensor_scalar(out=out.ap, in0=a.ap, scalar1=x1, scalar2=x2, op0=op0, op1=op1)
        rd = [a] + [x for x in (s1, s2) if isinstance(x, V)]
        P.add(eng, emit, reads=rd, writes=[out])

    def stt(out, a, s, b, op0, op1):
        def emit(e):
            x = s.ap if isinstance(s, V) else s
            return e.scalar_tensor_tensor(out=out.ap, in0=a.ap, scalar=x, in1=b.ap, op0=op0, op1=op1)
        rd = [a, b] + ([s] if isinstance(s, V) else [])
        P.add("dve", emit, reads=rd, writes=[out])

    def recip(x):
        P.add("dve", lambda e: e.reciprocal(out=x.ap, in_=x.ap), reads=[x], writes=[x])

    def dma(eng, outs, ins, sem, reads, writes):
        def emit(e):
            return [e.dma_start(out=o, in_=i) for o, i in zip(outs, ins)]
        P.add(eng, emit, reads=reads, writes=writes, kind="d", sem=sem, n=len(outs))

    def DR(key, lo=0, hi=1):
        return V(None, key, lo, hi)

    def vcol(col):
        return vecs[col:col + 1]

    dma("sp", [vecs[:].ap, ctab[:].ap], [vecs_d, tabs_d[:, T_C:T_C + 1280]], "const",
        reads=[], writes=[vecs[:], ctab[:]])
    act(identb[:], ctab[C_ID:C_ID + 128], AF.Copy)
    P.add("dve", lambda e: e.memset(onesb[:].ap, 1.0 / 1024.0), reads=[], writes=[onesb[:]])
    maskT = ctab[C_MASK:C_MASK + 128]
    P.add("dve", lambda e: e.memset(epsR[:].ap, RMS_EPS), reads=[], writes=[epsR[:]])
    P.add("dve", lambda e: e.memset(epsL[:].ap, LN_EPS), reads=[], writes=[epsL[:]])

    def w3(w, l):
        return w[l].rearrange("(kc p) n -> p kc n", p=128)

    def rmsnorm_stats(src, W, out_rstd):
        for c in range(8):
            act(sqb[c, 0:W], src[c, 0:W], AF.Square)
        ps = mm()
        pe_group(ps[0:W], [(onesb[:], sqb[c, 0:W]) for c in range(8)])
        act(out_rstd, ps[0:W], AF.Ln, bias=epsR[:])
        act(out_rstd, out_rstd, AF.Exp, scale=-0.5)

    def rmsnorm_hT(W, gcol, src=None):
        src = xt if src is None else src
        rstd = tmp[0, 0:W]
        rmsnorm_stats(src, W, rstd)
        for c in range(8):
            stt(hT[c, 0:W], src[c, 0:W], vcol(gcol + c), rstd, ALU.mult, ALU.mult)

    def load_tile(l, T):
        W, off = T.W, T.off
        src = xin if l == 0 else x1_d
        rd = [] if l == 0 else [DR("x1", off, off + W)]
        dma("sp", [xt[:, 0:W].ap], [fm(src)[:, :, off:off + W]], "xt", reads=rd, writes=[xt[:, 0:W]])
        tv = tabs_d[:, 0:2 * NTOK].rearrange("p (a t) -> p a t", a=2)
        dma("sp", [cs[:, 0:W].ap], [tv[:, :, off:off + W]], "cs", reads=[], writes=[cs[:, 0:W]])

    def proj_fm(wv, col, W, nk=8, rhs=None):
        ps = mm()
        src = hT if rhs is None else rhs
        pe_group(ps[0:W], [(wv(kc, col, col + 128), src[kc, 0:W]) for kc in range(nk)])
        return ps[0:W]

    def decay_evac(dst, ps, W, tabcol, h):
        if W == TW:
            o3 = V(dst.ap.rearrange("p (a b) -> p a b", a=4), dst.key, dst.lo, dst.hi)
            p3 = V(ps.ap.rearrange("p (a b) -> p a b", a=4), ps.key, ps.lo, ps.hi)
            tb = ctab[tabcol + h * 128: tabcol + (h + 1) * 128]
            t3 = V(tb.ap.unsqueeze(1).broadcast_to([128, 4, 128]), tb.key, tb.lo, tb.hi)
            tt(o3, p3, t3, ALU.mult)
        else:
            tt(dst, ps, ctab[tabcol + h * 128: tabcol + h * 128 + W], ALU.mult)

    def rope(x1d, x2d, o1, o2, W):
        c_, s_ = cs[0, 0:W], cs[1, 0:W]
        t1, t2 = tmp[4, 0:W], tmp[5, 0:W]
        tt(t1, x1d, c_, ALU.mult)
        tt(t2, x2d, s_, ALU.mult)
        tt(o1, t1, t2, ALU.subtract)
        tt(t1, x2d, c_, ALU.mult)
        tt(t2, x1d, s_, ALU.mult)
        tt(o2, t1, t2, ALU.add)

    def kq_proj(l, T, h, bs, pump=None):
        W = T.W
        key = ("in", l, "qk", h)
        wi = w3(w_in, l)
        wv = WS.get(key, [(wi[:, :, OFF_Q + h * 256: OFF_Q + (h + 1) * 256], 0),
                          (wi[:, :, OFF_K + h * 256: OFF_K + (h + 1) * 256], 256)], 8, 512)
        for idx in (2, 3, 0, 1):
            ps = proj_fm(wv, idx * 128, W)
            decay_evac(tmp[idx, 0:W], ps, W, C_DQ if idx < 2 else C_DK, h)
            if pump:
                pump(2)
        WS.done(key)

    def rope_k(T, h, bs, pump=None):
        W = T.W
        qk, kpp = qk_s[bs], kpp_s[bs]
        rope(tmp[2, 0:W], tmp[3, 0:W], qk[2, 0:W], qk[3, 0:W], W)
        if pump:
            pump(6)
        for tb, (s, n) in enumerate(T.chunks):
            pe_transposes([(ps7[dc * 128:(dc + 1) * 128].p(n), qk[2 + dc, s:s + n], identb[:]) for dc in range(2)])
            act(kpp[tb].p(n), ps7[0:256].p(n), AF.Copy, scale=T.gC[h])

    def rope_q(T, h, bs, pump=None):
        W = T.W
        qk = qk_s[bs]
        rope(tmp[0, 0:W], tmp[1, 0:W], qk[0, 0:W], qk[1, 0:W], W)
        if pump:
            pump(6)

    def k_side(l, T, h, bs, pump=None):
        kq_proj(l, T, h, bs, pump)
        rope_k(T, h, bs, pump)
        rope_q(T, h, bs, pump)

    def v_side(l, T, h, bs):
        vt = vt_s[bs]
        key = ("in", l, "v", h)
        wi = w3(w_in, l)
        wv = WS.get(key, [(wi[:, :, OFF_V + h * 512: OFF_V + (h + 1) * 512], 0)], 8, 512)
        for tb, (s, n) in enumerate(T.chunks):
            ps = mm()
            pe_group(ps[:].p(n), [(hT[kc, s:s + n], wv(kc, 0, 512)) for kc in range(8)])
            act(vt[tb].p(n), ps[:].p(n), AF.Copy)
        WS.done(key)

    def state_update(T, h, tb, n, first, bs=0):
        kpp, vt = kpp_s[bs], vt_s[bs]
        banks = [ps_o2[(tb + 1) % 2], ps_kv]
        for dc in range(2):
            pe_group(banks[dc][:], [(kpp[tb, dc * 128:(dc + 1) * 128].p(n), vt[tb].p(n))])
        for dc in range(2):
            stt(S[h * 2 + dc], S[h * 2 + dc], T.gC[h], banks[dc][:], ALU.mult, ALU.add)

    def ab_units(l, W, hsl, dst_of, tail_of=None):
        wi = w3(w_in, l)
        for u in range(4):
            key = ("in", l, "ab", u)
            wv = WS.get(key, [(wi[:, :, u * 256:(u + 1) * 256], 0), (wi[:, :, D + u * 256: D + (u + 1) * 256], 256)], 8, 512)
            for i in range(2):
                c = 2 * u + i
                psa, psb_ = mm(), mm()
                pe_group(psa[0:W], [(wv(kc, i * 128, (i + 1) * 128), hT[kc, hsl[0]:hsl[1]]) for kc in range(8)])
                pe_group(psb_[0:W], [(wv(kc, 256 + i * 128, 256 + (i + 1) * 128), hT[kc, hsl[0]:hsl[1]]) for kc in range(8)])
                sg = tmp[i, 0:W]
                act(sg, psb_[0:W], AF.Sigmoid)
                tt(dst_of(c), psa[0:W], sg, ALU.mult)
                if tail_of is not None:
                    tt(tail_of(c), psa[W - 30:W], tmp[i, W - 30:W], ALU.mult)
            WS.done(key)

    def phaseA(l, T, first_tile, last_tile):
        W = T.W
        vb = l * VL
        load_tile(l, T)
        rmsnorm_hT(W, vb + V_G1)
        for h in range(HEADS):
            k_side(l, T, h, with_q=False)
            v_side(l, T, h)
            for tb, (s, n) in enumerate(T.chunks):
                state_update(T, h, tb, n, first_tile and tb == 0)
        if last_tile:
            ab_units(l, 32, (W - 32, W), lambda c: gt[c])

    def prologue(l, T):
        load_tile(l, T)
        rmsnorm_hT(T.W, l * VL + V_G1)

    def phaseB(l, T, pro_done=False, nxt_tile=None):
        W, off = T.W, T.off
        vb = l * VL
        if not pro_done:
            prologue(l, T)
        act(glu[:, 0:30], halo[:, :], AF.Copy)
        ab_units(l, W, (0, W), lambda c: glu[c, 30:30 + W], lambda c: halo_n[c, :])
        act(halo[:, :], halo_n[:, :], AF.Copy)
        taps = [(c, j) for c in range(8) for j in range(31)]
        nxt = [0]
        pending = []

        def emit_tap():
            i = nxt[0]
            nxt[0] += 1
            c, j = taps[i]
            dg = dslots[i % 32]
            wcol = vcol(vb + V_CW + c * 31 + j)
            if i % 2 == 0:
                act(dg, identb[:], AF.Copy, scale=wcol)
            else:
                ts(dg, identb[:], wcol, None, ALU.mult)

            def fin(c=c, j=j, dg=dg):
                pe_acc(ps_conv[0:W], [(dg, glu[c, j:j + W])], start=(j == 0), stop=(j == 30))
                if j == 30:
                    act(accb[c, 0:W], ps_conv[0:W], AF.Identity, bias=vcol(vb + V_CB + c))
            pending.append(fin)

        def pump(k):
            for _ in range(k):
                if nxt[0] < len(taps):
                    emit_tap()
                if len(pending) > 6 or (nxt[0] >= len(taps) and pending):
                    pending.pop(0)()

        def conv_done():
            return nxt[0] >= len(taps) and not pending

        def conv_ln():
            while not conv_done():
                pump(1)
            for c in range(8):
                act(ycm[c, 0:W], accb[c, 0:W], AF.Copy)
                act(sqb[c, 0:W], accb[c, 0:W], AF.Square)
            psm, psq = mm(), mm()
            pe_group(psm[0:W], [(onesb[:], ycm[c, 0:W]) for c in range(8)])
            pe_group(psq[0:W], [(onesb[:], sqb[c, 0:W]) for c in range(8)])
            mean, var = tmp[2, 0:W], tmp[3, 0:W]
            act(mean, psm[0:W], AF.Copy)
            tt(var, mean, mean, ALU.mult)
            tt(var, psq[0:W], var, ALU.subtract)
            act(var, var, AF.Ln, bias=epsL[:])
            act(var, var, AF.Exp, scale=-0.5)
            for c in range(8):
                tt(accb[c, 0:W], accb[c, 0:W], mean, ALU.subtract)
                tt(accb[c, 0:W], accb[c, 0:W], var, ALU.mult)
                act(ycm[c, 0:W], accb[c, 0:W], AF.Silu, scale=vcol(vb + V_LNG + c), bias=vcol(vb + V_LNB + c))

        wi = w3(w_in, l)
        wc = w3(w_co, l)
        co_done = set()

        def conv_out_unit(u):
            co_done.add(u)
            k1, k2 = ("co", l, u), ("in", l, "gc", u)
            wv1 = WS.get(k1, [(wc[:, :, u * 512:(u + 1) * 512], 0)], 8, 512)
            wv2 = WS.get(k2, [(wi[:, :, OFF_GC + u * 512: OFF_GC + (u + 1) * 512], 0)], 8, 512)
            for i in range(4):
                oc = 4 * u + i
                pco = proj_fm(wv1, i * 128, W, rhs=ycm)
                pgc = proj_fm(wv2, i * 128, W)
                sg = tmp[i % 2, 0:W]
                act(sg, pgc, AF.Sigmoid)
                tt(mA[oc, 0:W], pco, sg, ALU.mult)
            WS.done(k1)
            WS.done(k2)

        nch = len(T.chunks)
        k_side(l, T, 0, 0)
        v_side(l, T, 0, 0)
        for h in range(HEADS):
            bs = h % 2
            qk, kpp, vt = qk_s[bs], kpp_s[bs], vt_s[bs]
            gkey = ("in", l, "g", h)
            gwv = WS.get(gkey, [(wi[:, :, OFF_G + h * 512: OFF_G + (h + 1) * 512], 0)], 8, 512)
            for tb, (s, n) in enumerate(T.chunks):
                pa = V(ps_att.ap[0:n, 0:n], ps_att.key, 0, 512)
                pe_group(pa, [(qk[2 + dc, s:s + n], qk[dc, s:s + n]) for dc in range(2)])
                at = V(attT.ap[0:n, 0:n], attT.key, 0, 256)
                mk = V(maskT.ap[0:n, 0:n], maskT.key, maskT.lo, maskT.hi)
                tt(at, pa, mk, ALU.mult)
                pump(3)
                pob = ps_o2[tb % 2]
                po = pob[:].p(n)
                pe_group(po, [(at, vt[tb].p(n))] + [(qk[dc, s:s + n], Sbf[h * 2 + dc]) for dc in range(2)])
                state_update(T, h, tb, n, False, bs)
                for dc in range(2):
                    act(Sbf[h * 2 + dc], S[h * 2 + dc], AF.Copy)
                pump(3)
                last = (h == HEADS - 1 and nch == 4)
                if nch != 4:
                    ecs = range(4)
                elif last:
                    ecs = {0: [0, 1], 1: [2, 3]}.get(tb, [])
                else:
                    ecs = [tb]
                for ec in ecs:
                    ps = proj_fm(gwv, ec * 128, W)
                    act(gate[ec, 0:W], ps, AF.Silu)
                if last and tb == 1:
                    WS.done(gkey)
                if h + 1 < HEADS:
                    if nch != 4:
                        k_side(l, T, h + 1, bs ^ 1, pump)
                        v_side(l, T, h + 1, bs ^ 1)
                    elif tb == 0:
                        kq_proj(l, T, h + 1, bs ^ 1, pump)
                    elif tb == 1:
                        v_side(l, T, h + 1, bs ^ 1)
                        rope_k(T, h + 1, bs ^ 1, pump)
                    elif tb == 2:
                        rope_q(T, h + 1, bs ^ 1, pump)
                if last and tb in (0, 2):
                    conv_out_unit(tb // 2)
                P.add("dve", lambda e, n=n, po=po: e.bn_stats(out=st6[:].ap[0:n], in_=po.ap), reads=[po], writes=[st6[:]])
                P.add("dve", lambda e, n=n: e.bn_aggr(out=mv[:].ap[0:n], in_=st6[:].ap[0:n]), reads=[st6[:]], writes=[mv[:]])
                act(rs[:].p(n), mv[1:2].p(n), AF.Ln, bias=epsL[:].p(n))
                act(rs[:].p(n), rs[:].p(n), AF.Exp, scale=-0.5)
                pump(2)
                stt(nmr[:].p(n), mv[0:1].p(n), -1.0, rs[:].p(n), ALU.mult, ALU.mult)
                ts(on[tb].p(n), po, rs[:].p(n), nmr[:].p(n), ALU.mult, ALU.add)
                pump(4)
            if not (h == HEADS - 1 and nch == 4):
                WS.done(gkey)
            for ec in range(4):
                pe_transposes([(V(ps7.ap[:, 512 + s:512 + s + n], ps7.key, (512 + s) * 2, (512 + s + n) * 2),
                                on[tb, ec * 128:(ec + 1) * 128].p(n),
                                V(identb.ap[0:n, 0:n], identb.key, 0, 256)) for tb, (s, n) in enumerate(T.chunks)])
                stt(retin[h * 4 + ec, 0:W], ps7[512:512 + W], vcol(vb + V_GNG + h * 4 + ec), gate[ec, 0:W], ALU.mult, ALU.mult)
                pump(3)
            if h == 2:
                conv_ln()
        for u in range(2):
            if u not in co_done:
                conv_out_unit(u)
        wr = w3(w_ro, l)
        for u in range(4):
            kr = ("ro", l, u)
            wvr = WS.get(kr, [(wr[:, :, u * 256:(u + 1) * 256], 0)], 16, 256)
            if u % 2 == 0:
                kg = ("in", l, "gr", u // 2)
                wvg = WS.get(kg, [(wi[:, :, OFF_GR + (u // 2) * 512: OFF_GR + (u // 2 + 1) * 512], 0)], 8, 512)
            for i in range(2):
                oc = 2 * u + i
                pro = proj_fm(wvr, i * 128, W, nk=16, rhs=retin)
                pgr = proj_fm(wvg, (oc % 4) * 128, W)
                sg = tmp[i, 0:W]
                act(sg, pgr, AF.Sigmoid)
                t2 = tmp[2 + i, 0:W]
                tt(t2, pro, sg, ALU.mult)
                tt(ycm[oc, 0:W], t2, mA[oc, 0:W], ALU.add)
            WS.done(kr)
            if u % 2 == 1:
                WS.done(kg)
        wo = w3(w_o, l)
        for u in range(2):
            key = ("wo", l, u)
            wv = WS.get(key, [(wo[:, :, u * 512:(u + 1) * 512], 0)], 8, 512)
            for i in range(4):
                oc = 4 * u + i
                ps = proj_fm(wv, i * 128, W, rhs=ycm)
                tt(xo[oc, 0:W], xt[oc, 0:W], ps, ALU.add)
            WS.done(key)
        rmsnorm_hT(W, vb + V_G2, src=xo)
        w1 = w3(w_m1, l)
        for u in range(8):
            key = ("m1", l, u)
            wv = WS.get(key, [(w1[:, :, u * 512:(u + 1) * 512], 0)], 8, 512)
            for i in range(4):
                fc = 4 * u + i
                ps = proj_fm(wv, i * 128, W)
                r = tmp[1 + (i % 2), 0:W]
                act(r, ps, AF.Relu)
                act(ubuf[fc, 0:W], r, AF.Square)
            WS.done(key)
        if nxt_tile is not None:
            prologue(*nxt_tile)
        w2 = w3(w_m2, l)
        for cp in range(4):
            pss = [mm(), mm()]
            for hk in range(2):
                key = ("m2", l, cp, hk)
                wv = WS.get(key, [(w2[:, hk * 16:(hk + 1) * 16, cp * 256:(cp + 1) * 256], 0)], 16, 256)
                for i in range(2):
                    pe_acc(pss[i][0:W], [(wv(kc, i * 128, (i + 1) * 128), ubuf[hk * 16 + kc, 0:W]) for kc in range(16)],
                           start=(hk == 0), stop=(hk == 1))
                WS.done(key)
            for i in range(2):
                oc = 2 * cp + i
                tt(xo[oc, 0:W], xo[oc, 0:W], pss[i][0:W], ALU.add)
        if l == 0:
            dma("sp", [fm(x1_d)[:, :, off:off + W]], [xo[:, 0:W].ap], ("x1w", (off // TW) % 4), reads=[xo[:, 0:W]],
                writes=[DR("x1", off, off + W)])
        else:
            rstd = tmp[0, 0:W]
            rmsnorm_stats(xo, W, rstd)
            for c in range(8):
                stt(accb[c, 0:W], xo[c, 0:W], vcol(V_FG + c), rstd, ALU.mult, ALU.mult)
            dma("sp", [fm(y_d)[:, :, off:off + W]], [accb[:, 0:W].ap], "out", reads=[accb[:, 0:W]], writes=[DR("y", off, off + W)])

    def s_dram(ap_l):
        return ap_l.rearrange("h (dc p) e -> p (h dc) e", p=128)

    ptiles = [Tile_(t * TW, TW, [(i * 128, 128) for i in range(4)], False) for t in range(NPT)]
    stile = Tile_(SEG, SW, [(0, SW)], True)

    order = []
    for l in range(2):
        order.append((l, stile))
        order += [(l, T) for T in ptiles]
    pro_done = False
    for idx, (l, T) in enumerate(order):
        nxt = order[idx + 1] if idx + 1 < len(order) else None
        if T.sample:
            P.epoch += 1
            dma("sp", [S[:].ap, halo[:].ap], [s_dram(strt[l]), fm(stc[l])], "sld", reads=[], writes=[S[:], halo[:]])
            for hd in range(8):
                act(Sbf[hd], S[hd], AF.Copy)
            phaseB(l, T, pro_done, nxt)
            dma("sp", [s_dram(nsr_s[l]), fm(ncs_s[l])], [S[:].ap, halo[:].ap], "out", reads=[S[:], halo[:]], writes=[DR(("os", l))])
            P.add("dve", lambda e: e.memset(S[:].ap, 0.0), reads=[], writes=[S[:]])
            P.add("dve", lambda e: e.memset(Sbf[:].ap, 0.0), reads=[], writes=[Sbf[:]])
            P.add("dve", lambda e: e.memset(halo[:].ap, 0.0), reads=[], writes=[halo[:]])
        else:
            t = T.off // TW
            if t % 4 == 0:
                P.epoch += 1
            phaseB(l, T, pro_done, nxt)
            if t == NPT - 1:
                dma("sp", [s_dram(nsr_p[l]), fm(ncs_p[l])], [S[:].ap, halo[:].ap], "out", reads=[S[:], halo[:]], writes=[DR(("op", l))])
        pro_done = nxt is not None
    return P, WS


def emit_program(nc, P):
    ops = P.ops
    eng_ops = {e: [] for e in ("pe", "act", "dve", "pool", "sp")}
    pos = {}
    for i, o in enumerate(ops):
        pos[i] = len(eng_ops[o["eng"]])
        eng_ops[o["eng"]].append(i)
    need = set()
    for i, o in enumerate(ops):
        for d in o["deps"]:
            a = ops[d]
            if a["kind"] != "c":
                continue
            if a["eng"] != o["eng"]:
                need.add(d)
            elif a["eng"] != "pe" and pos[i] - pos[d] <= 2:
                need.add(d)
    sigidx = {}
    for e, lst in eng_ops.items():
        kk = {}
        for i in lst:
            if i in need:
                ep = ops[i]["ep"]
                kk[ep] = kk.get(ep, 0) + 1
                sigidx[i] = kk[ep]
    sems = {}

    def sem(key):
        if key not in sems:
            sems[key] = nc.alloc_semaphore("s_" + "_".join(str(x) for x in (key if isinstance(key, tuple) else (key,))))
        return sems[key]
    final_waits = {}
    for o in ops:
        if o["kind"] != "c":
            final_waits[o["sem"]] = o["cum"]

    def run_engine(ename, e):
        waited = {}
        for i in eng_ops[ename]:
            o = ops[i]
            req = {}
            for d in o["deps"]:
                a = ops[d]
                if a["kind"] == "c":
                    if a["eng"] == ename:
                        if ename == "pe" or pos[i] - pos[d] > 2:
                            continue
                    k, v = ("eng", a["eng"], a["ep"]), sigidx[d]
                else:
                    k, v = a["sem"], a["cum"]
                if v > req.get(k, 0):
                    req[k] = v
            for k, v in req.items():
                if v > waited.get(k, 0):
                    e.wait_ge(sem(k), v)
                    waited[k] = v
            r = o["emit"](e)
            if o["kind"] == "c":
                if i in need:
                    r.then_inc(sem(("eng", ename, o["ep"])), 1)
            elif o["kind"] == "d":
                for ins in r:
                    ins.then_inc(sem(o["sem"]), 16)
            else:
                r.then_inc(sem(o["sem"]))
        if ename == "sp":
            for k, v in final_waits.items():
                if v > waited.get(k, 0):
                    e.wait_ge(sem(k), v)

    with nc.Block() as block:
        @block.tensor
        def _(e):
            run_engine("pe", e)

        @block.scalar
        def _(e):
            run_engine("act", e)

        @block.vector
        def _(e):
            run_engine("dve", e)

        @block.gpsimd
        def _(e):
            run_engine("pool", e)

        @block.sync
        def _(e):
            run_engine("sp", e)


def build_nc():
    nc = bass.Bass("TRN2", target_bir_lowering=False)
    cache = {}
    _, ws0 = build_program(nc, True, None, cache)
    P, WS = build_program(nc, False, ws0.rec, cache)
    assert WS.i == len(ws0.rec) and WS.issued == len(ws0.rec)
    emit_program(nc, P)
    return nc


def _host_tables(core):
    pos = np.concatenate([np.arange(SEG), PAST + np.arange(SW)]).astype(np.float32)
    inv = (10000.0 ** (-np.arange(0, 256, 2, dtype=np.float32) / np.float32(256))).astype(np.float32)
    ang = (pos[None, :] * inv[:, None]).astype(np.float32)
    tabs = np.zeros((128, NTAB), np.float32)
    tabs[:, T_COS:T_COS + NTOK] = np.cos(ang)
    tabs[:, T_SIN:T_SIN + NTOK] = np.sin(ang)
    n = np.arange(128, dtype=np.float64)
    for h in range(HEADS):
        g = GAMMA[h]
        tabs[:, T_C + C_DQ + h * 128: T_C + C_DQ + (h + 1) * 128] = (g ** (n + 1.0))[None, :]
        tabs[:, T_C + C_DK + h * 128: T_C + C_DK + (h + 1) * 128] = (g ** (-(n + 1.0)) / 16.0)[None, :]
    m = np.arange(128)
    tabs[:, T_C + C_MASK: T_C + C_MASK + 128] = (m[None, :] >= m[:, None]).astype(np.float32)
    tabs[:, T_C + C_ID: T_C + C_ID + 128] = np.eye(128, dtype=np.float32)
    coef = np.zeros((128, 20), np.float32)
    return tabs, coef


def _pcol(v):
    return np.ascontiguousarray(np.asarray(v, np.float32).reshape(-1, 128).T)


_NC_CACHE = {}


def kernel(x_prompt, x_sample, state_conv, state_ret, norm1_g, w_in, conv_w, conv_b, conv_ln_g, conv_ln_b,
           w_conv_out, ret_gn_g, w_ret_out, w_out, norm2_g, w_mlp1, w_mlp2, final_g):
    f = lambda a: np.ascontiguousarray(np.asarray(a, dtype=np.float32))
    x_prompt, x_sample, state_conv, state_ret = f(x_prompt), f(x_sample), f(state_conv), f(state_ret)
    vecs = np.zeros((128, NV), np.float32)
    for l in range(2):
        b = l * VL
        vecs[:, b + V_G1:b + V_G1 + 8] = _pcol(norm1_g[l])
        vecs[:, b + V_G2:b + V_G2 + 8] = _pcol(norm2_g[l])
        vecs[:, b + V_CB:b + V_CB + 8] = _pcol(conv_b[l])
        vecs[:, b + V_LNG:b + V_LNG + 8] = _pcol(conv_ln_g[l])
        vecs[:, b + V_LNB:b + V_LNB + 8] = _pcol(conv_ln_b[l])
        vecs[:, b + V_GNG:b + V_GNG + 16] = _pcol(ret_gn_g[l])
        cw = np.asarray(conv_w[l], np.float32)
        vecs[:, b + V_CW:b + V_CW + 248] = cw.T.reshape(8, 128, 31).transpose(1, 0, 2).reshape(128, 248)
    vecs[:, V_FG:V_FG + 8] = _pcol(final_g)
    shared = dict(w_in=f(w_in), w_co=f(w_conv_out), w_ro=f(w_ret_out), w_o=f(w_out), w_m1=f(w_mlp1), w_m2=f(w_mlp2), vecs=vecs)
    in_maps = []
    for c in range(NCORES):
        xin = np.concatenate([x_prompt[c % 2].T, x_sample[c].T], axis=1)
        tabs, coef = _host_tables(c)
        m = dict(shared)
        m.update(xin=np.ascontiguousarray(xin), stc=np.ascontiguousarray(state_conv[:, c].transpose(0, 2, 1)),
                 strt=np.ascontiguousarray(state_ret[:, c]), tabs=tabs, coef=coef)
        in_maps.append(m)
    if "nc" not in _NC_CACHE:
        _NC_CACHE["nc"] = build_nc()
    res = run_bass_kernel_spmd(_NC_CACHE["nc"], in_maps, core_ids=list(range(NCORES)))
    R = res.results
    y_p = np.zeros((2, SEQ, D), np.float32)
    y_s = np.zeros((8, SW, D), np.float32)
    for c in range(NCORES):
        yc = R[c]["y"]
        if c < 2:
            y_p[c] = yc[:, :SEG].T
        y_s[c] = yc[:, SEG:].T
    ncs_p = np.stack([R[b]["ncs_p"].transpose(0, 2, 1) for b in range(2)], axis=1)
    nsr_p = np.stack([R[b]["nsr_p"] for b in range(2)], axis=1)
    ncs_s = np.stack([R[c]["ncs_s"].transpose(0, 2, 1) for c in range(NCORES)], axis=1)
    nsr_s = np.stack([R[c]["nsr_s"] for c in range(NCORES)], axis=1)
    return (y_p, y_s, np.ascontiguousarray(ncs_p), np.ascontiguousarray(nsr_p),
            np.ascontiguousarray(ncs_s), np.ascontiguousarray(nsr_s))
```

```python
import os
import numpy as np
import concourse.bass as bass
import concourse.mybir as mybir
from concourse.bass_utils import run_bass_kernel_spmd

F32 = mybir.dt.float32
BF16 = mybir.dt.bfloat16
ALU = mybir.AluOpType
AF = mybir.ActivationFunctionType

NCORES = 8
D = 1024
SEQ = 8192
SEG = 8192
TW = 512
NPT = SEG // TW
SW = 32
NTOK = SEG + SW
PAST = 4096
HEADS = 4
DIN = 10240
OFF_Q, OFF_K, OFF_V, OFF_G, OFF_GC, OFF_GR = 2048, 3072, 4096, 6144, 8192, 9216
RMS_EPS = 1e-6
LN_EPS = 1e-5
NB = 4
NTMP = 6
GAMMA = [1.0 - 2.0 ** (-5 - h) for h in range(HEADS)]
VL = 304
V_G1, V_G2, V_CB, V_LNG, V_LNB, V_GNG, V_CW = 0, 8, 16, 24, 32, 40, 56
V_FG = 2 * VL
NV = 2 * VL + 8
C_DQ, C_DK, C_MASK, C_ID = 0, 512, 1024, 1152
T_COS, T_SIN, T_C = 0, NTOK, 2 * NTOK
NTAB = 2 * NTOK + 1280


class V:
    __slots__ = ("ap", "key", "lo", "hi")

    def __init__(self, ap, key, lo, hi):
        self.ap, self.key, self.lo, self.hi = ap, key, lo, hi

    def p(self, n):
        return V(self.ap[0:n], self.key, self.lo, self.hi)


class Buf:
    def __init__(self, key, ap, shape, esize, base=0):
        self.key, self.shape, self.esize, self.base = key, tuple(shape), esize, base
        n = int(np.prod(shape))
        self.flat = ap
        if len(shape) == 2:
            ap = ap.rearrange("p (a b) -> p a b", a=shape[0])
        elif len(shape) == 3:
            ap = ap.rearrange("p (a b c) -> p a b c", a=shape[0], b=shape[1])
        self.ap = ap
        st, acc = [], 1
        for s in reversed(shape):
            st.append(acc)
            acc *= s
        self.strides = list(reversed(st))
        self.n = n

    def __getitem__(self, idx):
        if not isinstance(idx, tuple):
            idx = (idx,)
        idx = list(idx) + [slice(None)] * (len(self.shape) - len(idx))
        lo = hi = 0
        for i, s, st in zip(idx, self.shape, self.strides):
            if isinstance(i, slice):
                a = 0 if i.start is None else i.start
                b = s if i.stop is None else i.stop
            else:
                a, b = i, i + 1
            assert 0 <= a < b <= s, (self.key, idx, self.shape)
            lo += a * st
            hi += (b - 1) * st
        ap = self.ap[(slice(None),) + tuple(idx)]
        return V(ap, self.key, self.base + lo * self.esize, self.base + (hi + 1) * self.esize)


class Prog:
    def __init__(self, dry):
        self.dry = dry
        self.ops = []
        self.acc = {}
        self.dcount = {}
        self.epoch = 0

    def add(self, eng, emit, reads=(), writes=(), kind="c", sem=None, n=1):
        if self.dry:
            return
        deps = set()
        oid = len(self.ops)
        for v in reads:
            for r in self.acc.get(v.key, ()):
                if r[3] and r[0] < v.hi and v.lo < r[1]:
                    deps.add(r[2])
        for v in writes:
            for r in self.acc.get(v.key, ()):
                if r[0] < v.hi and v.lo < r[1]:
                    deps.add(r[2])
        for v in writes:
            lst = [r for r in self.acc.get(v.key, []) if not (v.lo <= r[0] and r[1] <= v.hi)]
            lst.append((v.lo, v.hi, oid, True))
            self.acc[v.key] = lst
        for v in reads:
            self.acc.setdefault(v.key, []).append((v.lo, v.hi, oid, False))
        deps.discard(oid)
        cum = None
        if kind != "c":
            cum = self.dcount.get(sem, 0) + (16 * n if kind == "d" else 1)
            self.dcount[sem] = cum
        self.ops.append(dict(eng=eng, emit=emit, deps=deps, kind=kind, sem=sem, cum=cum, ep=self.epoch))


class WStream:
    def __init__(self, P, slots, seq):
        self.P, self.slots = P, slots
        self.rec = [] if seq is None else None
        self.seq = seq
        self.i = 0
        self.issued = 0
        self.open = {}
        self.finished = set()
        if seq is not None:
            for _ in range(min(NB, len(seq))):
                self._issue()

    def _issue(self):
        i = self.issued
        key, pieces, nk, nc_ = self.seq[i]
        slot = self.slots[i % NB]
        wr = slot[0:nk * nc_]
        view = wr.ap.rearrange("p (k n) -> p k n", k=nk)

        def emit(e, pieces=pieces, view=view):
            return [e.dma_start(out=view[:, :, c0:c0 + src.shape[2]], in_=src) for (src, c0) in pieces]
        self.P.add("pool", emit, reads=[], writes=[slot[:]], kind="d", sem=("w", i % NB), n=len(pieces))
        self.issued += 1

    def get(self, key, pieces, nk, nc_):
        i = self.i
        self.i += 1
        if self.rec is not None:
            self.rec.append((key, pieces, nk, nc_))
        else:
            assert self.seq[i][0] == key, (self.seq[i][0], key)
            assert i < self.issued, ("weight unit requested before its slot was free", key, sorted(self.open))
        slot = self.slots[i % NB]
        self.open[key] = i

        def wv(kc, c0, c1, slot=slot, nc_=nc_):
            return slot[kc * nc_ + c0: kc * nc_ + c1]
        return wv

    def done(self, key):
        i = self.open.pop(key)
        self.finished.add(i)
        if self.seq is not None:
            while self.issued < len(self.seq) and (self.issued - NB) in self.finished:
                self._issue()


class Tile_:
    def __init__(self, off, W, chunks, sample):
        self.off, self.W, self.chunks, self.sample = off, W, chunks, sample
        self.gC = [g ** chunks[0][1] for g in GAMMA]


def build_program(nc, dry, seq, cache):
    P = Prog(dry)

    def memo(kind, name, fn):
        k = (kind, name)
        if k not in cache:
            cache[k] = fn()
        return cache[k]

    def dt(name, shape, dtype, **kw):
        return memo("d", name, lambda: nc.dram_tensor(name, shape, dtype, **kw))
    xin = dt("xin", [D, NTOK], F32, kind="ExternalInput").ap()
    stc = dt("stc", [2, D, 30], F32, kind="ExternalInput").ap()
    strt = dt("strt", [2, HEADS, 256, 512], F32, kind="ExternalInput").ap()
    w_in = dt("w_in", [2, D, DIN], F32, kind="ExternalInput").ap()
    w_co = dt("w_co", [2, D, D], F32, kind="ExternalInput").ap()
    w_ro = dt("w_ro", [2, 2 * D, D], F32, kind="ExternalInput").ap()
    w_o = dt("w_o", [2, D, D], F32, kind="ExternalInput").ap()
    w_m1 = dt("w_m1", [2, D, 4 * D], F32, kind="ExternalInput").ap()
    w_m2 = dt("w_m2", [2, 4 * D, D], F32, kind="ExternalInput").ap()
    vecs_d = dt("vecs", [128, NV], F32, kind="ExternalInput").ap()
    tabs_d = dt("tabs", [128, NTAB], F32, kind="ExternalInput").ap()
    coef_d = dt("coef", [128, 20], F32, kind="ExternalInput").ap()
    y_d = dt("y", [D, NTOK], F32, kind="ExternalOutput").ap()
    ncs_p = dt("ncs_p", [2, D, 30], F32, kind="ExternalOutput").ap()
    nsr_p = dt("nsr_p", [2, HEADS, 256, 512], F32, kind="ExternalOutput").ap()
    ncs_s = dt("ncs_s", [2, D, 30], F32, kind="ExternalOutput").ap()
    nsr_s = dt("nsr_s", [2, HEADS, 256, 512], F32, kind="ExternalOutput").ap()
    x1_d = dt("x1s", [D, NTOK], F32).ap()
    loc_d = [dt(f"loc{l}", [128, 4352], F32).ap() for l in range(2)]
    gath_d = [dt(f"gath{l}", [4 * 128, 4352], F32).ap() for l in range(2)]

    def fm(ap):
        return ap.rearrange("(c p) t -> p c t", p=128)

    def sb(name, shape, dtype):
        es = 4 if dtype == F32 else 2
        n = int(np.prod(shape))
        t = memo("s", name, lambda: nc.alloc_sbuf_tensor("sb_" + name, [128, n], dtype)).ap()
        return Buf(name, t, shape, es)

    xt = sb("xt", [8, TW], F32)
    hT = sb("hT", [8, TW], BF16)
    sqb = sb("sqb", [8, TW], BF16)
    tmp = sb("tmp", [NTMP, TW], F32)
    dslots = Buf("sqb", sqb.flat, [32, 128], 2)
    arena_ap = memo("s", "arena", lambda: nc.alloc_sbuf_tensor("arena", [128, 8448], F32)).ap()
    glu = Buf("arena", arena_ap[:, 0:4 * 544].bitcast(BF16), [8, 544], 2, 0)
    accb = Buf("arena", arena_ap[:, 4352:4352 + 4096], [8, TW], 4, 4352 * 4)
    ubuf = Buf("arena", arena_ap[:, 0:8192].bitcast(BF16), [32, TW], 2, 0)
    retin = sb("retin", [16, TW], BF16)
    tmpS = Buf("arena", arena_ap[:, 0:4352], [4352], 4, 0)
    xo = Buf("retin", retin.flat.bitcast(F32), [8, TW], 4)
    ycm = sb("ycm", [8, TW], BF16)
    mA = sb("mA", [8, TW], BF16)
    qk_s = [sb(f"qk{i}", [4, TW], BF16) for i in range(2)]
    kpp_s = [sb(f"kpp{i}", [4, 256], BF16) for i in range(2)]
    vt_s = [sb(f"vt{i}", [4, 512], BF16) for i in range(2)]
    gate = sb("gate", [4, TW], BF16)
    attT = sb("attT", [128], BF16)
    on = sb("on", [4, 512], BF16)
    S = sb("S", [8, 512], F32)
    Sbf = sb("Sbf", [8, 512], BF16)
    halo = sb("halo", [8, 30], F32)
    halo_n = sb("halo_n", [8, 30], F32)
    cs = sb("cs", [2, TW], F32)
    ctab = sb("ctab", [1280], F32)
    identb = sb("identb", [128], BF16)
    onesb = sb("onesb", [128], BF16)
    vecs = sb("vecs", [NV], F32)
    st6 = sb("st6", [6], F32)
    mv = sb("mv", [2], F32)
    rs = sb("rs", [1], F32)
    nmr = sb("nmr", [1], F32)
    epsR = sb("epsR", [1], F32)
    epsL = sb("epsL", [1], F32)
    wslots = [sb(f"wb{i}", [4096], BF16) for i in range(NB)]

    psb = [Buf(f"ps{i}", memo("p", f"ps{i}", lambda i=i: nc.alloc_psum_tensor(f"ps{i}", [128, 512], F32)).ap(), [512], 4) for i in range(7)]
    ps7 = Buf("ps7", memo("p", "ps7", lambda: nc.alloc_psum_tensor("ps7", [128, 1024], BF16)).ap(), [1024], 2)
    mmi = [0]

    def mm():
        b = psb[mmi[0] % 2]
        mmi[0] += 1
        return b
    ps_att, ps_o2, ps_kv = psb[4], [psb[3], psb[5]], psb[6]
    ps_conv = psb[2]

    WS = WStream(P, wslots, seq)

    def pe_group(out, pairs):
        n = len(pairs)

        def emit(e, out=out, pairs=pairs, n=n):
            ins = None
            for i, (l, r) in enumerate(pairs):
                ins = e.matmul(out.ap, l.ap, r.ap, start=(i == 0), stop=(i == n - 1))
            return ins
        P.add("pe", emit, reads=[v for pr in pairs for v in pr], writes=[out])

    def pe_acc(out, pairs, start, stop):
        n = len(pairs)

        def emit(e):
            ins = None
            for i, (l, r) in enumerate(pairs):
                ins = e.matmul(out.ap, l.ap, r.ap, start=(start and i == 0), stop=(stop and i == n - 1))
            return ins
        P.add("pe", emit, reads=[v for pr in pairs for v in pr], writes=[out])

    def pe_transposes(items):
        def emit(e):
            ins = None
            for (o, i_, idn) in items:
                ins = e.transpose(o.ap, i_.ap, idn.ap)
            return ins
        P.add("pe", emit, reads=[x for it in items for x in (it[1], it[2])], writes=[it[0] for it in items])

    def act(out, in_, func, scale=1.0, bias=None, extra_reads=()):
        def emit(e):
            kw = {}
            if bias is not None:
                kw["bias"] = bias.ap if isinstance(bias, V) else bias
            sc = scale.ap if isinstance(scale, V) else scale
            return e.activation(out=out.ap, in_=in_.ap, func=func, scale=sc, **kw)
        rd = [in_] + [x for x in (scale, bias) if isinstance(x, V)] + list(extra_reads)
        P.add("act", emit, reads=rd, writes=[out])

    def tt(out, a, b, op):
        P.add("dve", lambda e: e.tensor_tensor(out=out.ap, in0=a.ap, in1=b.ap, op=op), reads=[a, b], writes=[out])

    def ts(out, a, s1, s2, op0, op1=None, eng="dve"):
        def emit(e):
            x1 = s1.ap if isinstance(s1, V) else s1
            x2 = s2.ap if isinstance(s2, V) else s2
            if op1 is None:
                return e.tensor_scalar(out=out.ap, in0=a.ap, scalar1=x1, scalar2=None, op0=op0)
            return e.tensor_scalar(out=out.ap, in0=a.ap, scalar1=x1, scalar2=x2, op0=op0, op1=op1)
        rd = [a] + [x for x in (s1, s2) if isinstance(x, V)]
        P.add(eng, emit, reads=rd, writes=[out])

    def stt(out, a, s, b, op0, op1):
        def emit(e):
            x = s.ap if isinstance(s, V) else s
            return e.scalar_tensor_tensor(out=out.ap, in0=a.ap, scalar=x, in1=b.ap, op0=op0, op1=op1)
        rd = [a, b] + ([s] if isinstance(s, V) else [])
        P.add("dve", emit, reads=rd, writes=[out])

    def recip(x):
        P.add("dve", lambda e: e.reciprocal(out=x.ap, in_=x.ap), reads=[x], writes=[x])

    def dma(eng, outs, ins, sem, reads, writes):
        def emit(e):
            return [e.dma_start(out=o, in_=i) for o, i in zip(outs, ins)]
        P.add(eng, emit, reads=reads, writes=writes, kind="d", sem=sem, n=len(outs))

    def DR(key, lo=0, hi=1):
        return V(None, key, lo, hi)

    def vcol(col):
        return vecs[col:col + 1]

    dma("sp", [vecs[:].ap, ctab[:].ap], [vecs_d, tabs_d[:, T_C:T_C + 1280]], "const",
        reads=[], writes=[vecs[:], ctab[:]])
    act(identb[:], ctab[C_ID:C_ID + 128], AF.Copy)
    P.add("dve", lambda e: e.memset(onesb[:].ap, 1.0 / 1024.0), reads=[], writes=[onesb[:]])
    maskT = ctab[C_MASK:C_MASK + 128]
    P.add("dve", lambda e: e.memset(epsR[:].ap, RMS_EPS), reads=[], writes=[epsR[:]])
    P.add("dve", lambda e: e.memset(epsL[:].ap, LN_EPS), reads=[], writes=[epsL[:]])

    def w3(w, l):
        return w[l].rearrange("(kc p) n -> p kc n", p=128)

    def rmsnorm_stats(src, W, out_rstd):
        for c in range(8):
            act(sqb[c, 0:W], src[c, 0:W], AF.Square)
        ps = mm()
        pe_group(ps[0:W], [(onesb[:], sqb[c, 0:W]) for c in range(8)])
        act(out_rstd, ps[0:W], AF.Ln, bias=epsR[:])
        act(out_rstd, out_rstd, AF.Exp, scale=-0.5)

    def rmsnorm_hT(W, gcol, src=None):
        src = xt if src is None else src
        rstd = tmp[0, 0:W]
        rmsnorm_stats(src, W, rstd)
        for c in range(8):
            stt(hT[c, 0:W], src[c, 0:W], vcol(gcol + c), rstd, ALU.mult, ALU.mult)

    def load_tile(l, T):
        W, off = T.W, T.off
        src = xin if l == 0 else x1_d
        rd = [] if l == 0 else [DR("x1", off, off + W)]
        dma("sp", [xt[:, 0:W].ap], [fm(src)[:, :, off:off + W]], "xt", reads=rd, writes=[xt[:, 0:W]])
        tv = tabs_d[:, 0:2 * NTOK].rearrange("p (a t) -> p a t", a=2)
        dma("sp", [cs[:, 0:W].ap], [tv[:, :, off:off + W]], "cs", reads=[], writes=[cs[:, 0:W]])

    def proj_fm(wv, col, W, nk=8, rhs=None):
        ps = mm()
        src = hT if rhs is None else rhs
        pe_group(ps[0:W], [(wv(kc, col, col + 128), src[kc, 0:W]) for kc in range(nk)])
        return ps[0:W]

    def decay_evac(dst, ps, W, tabcol, h):
        if W == TW:
            o3 = V(dst.ap.rearrange("p (a b) -> p a b", a=4), dst.key, dst.lo, dst.hi)
            p3 = V(ps.ap.rearrange("p (a b) -> p a b", a=4), ps.key, ps.lo, ps.hi)
            tb = ctab[tabcol + h * 128: tabcol + (h + 1) * 128]
            t3 = V(tb.ap.unsqueeze(1).broadcast_to([128, 4, 128]), tb.key, tb.lo, tb.hi)
            tt(o3, p3, t3, ALU.mult)
        else:
            tt(dst, ps, ctab[tabcol + h * 128: tabcol + h * 128 + W], ALU.mult)

    def rope(x1d, x2d, o1, o2, W):
        c_, s_ = cs[0, 0:W], cs[1, 0:W]
        t1, t2 = tmp[4, 0:W], tmp[5, 0:W]
        tt(t1, x1d, c_, ALU.mult)
        tt(t2, x2d, s_, ALU.mult)
        tt(o1, t1, t2, ALU.subtract)
        tt(t1, x2d, c_, ALU.mult)
        tt(t2, x1d, s_, ALU.mult)
        tt(o2, t1, t2, ALU.add)

    def kq_proj(l, T, h, bs, pump=None):
        W = T.W
        key = ("in", l, "qk", h)
        wi = w3(w_in, l)
        wv = WS.get(key, [(wi[:, :, OFF_Q + h * 256: OFF_Q + (h + 1) * 256], 0),
                          (wi[:, :, OFF_K + h * 256: OFF_K + (h + 1) * 256], 256)], 8, 512)
        for idx in (2, 3, 0, 1):
            ps = proj_fm(wv, idx * 128, W)
            decay_evac(tmp[idx, 0:W], ps, W, C_DQ if idx < 2 else C_DK, h)
            if pump:
                pump(2)
        WS.done(key)

    def rope_k(T, h, bs, pump=None):
        W = T.W
        qk, kpp = qk_s[bs], kpp_s[bs]
        rope(tmp[2, 0:W], tmp[3, 0:W], qk[2, 0:W], qk[3, 0:W], W)
        if pump:
            pump(6)
        for tb, (s, n) in enumerate(T.chunks):
            pe_transposes([(ps7[dc * 128:(dc + 1) * 128].p(n), qk[2 + dc, s:s + n], identb[:]) for dc in range(2)])
            act(kpp[tb].p(n), ps7[0:256].p(n), AF.Copy, scale=T.gC[h])

    def rope_q(T, h, bs, pump=None):
        W = T.W
        qk = qk_s[bs]
        rope(tmp[0, 0:W], tmp[1, 0:W], qk[0, 0:W], qk[1, 0:W], W)
        if pump:
            pump(6)

    def k_side(l, T, h, bs, pump=None):
        kq_proj(l, T, h, bs, pump)
        rope_k(T, h, bs, pump)
        rope_q(T, h, bs, pump)

    def v_side(l, T, h, bs):
        vt = vt_s[bs]
        key = ("in", l, "v", h)
        wi = w3(w_in, l)
        wv = WS.get(key, [(wi[:, :, OFF_V + h * 512: OFF_V + (h + 1) * 512], 0)], 8, 512)
        for tb, (s, n) in enumerate(T.chunks):
            ps = mm()
            pe_group(ps[:].p(n), [(hT[kc, s:s + n], wv(kc, 0, 512)) for kc in range(8)])
            act(vt[tb].p(n), ps[:].p(n), AF.Copy)
        WS.done(key)

    def state_update(T, h, tb, n, first, bs=0):
        kpp, vt = kpp_s[bs], vt_s[bs]
        banks = [ps_o2[(tb + 1) % 2], ps_kv]
        for dc in range(2):
            pe_group(banks[dc][:], [(kpp[tb, dc * 128:(dc + 1) * 128].p(n), vt[tb].p(n))])
        for dc in range(2):
            stt(S[h * 2 + dc], S[h * 2 + dc], T.gC[h], banks[dc][:], ALU.mult, ALU.add)

    def ab_units(l, W, hsl, dst_of, tail_of=None):
        wi = w3(w_in, l)
        for u in range(4):
            key = ("in", l, "ab", u)
            wv = WS.get(key, [(wi[:, :, u * 256:(u + 1) * 256], 0), (wi[:, :, D + u * 256: D + (u + 1) * 256], 256)], 8, 512)
            for i in range(2):
                c = 2 * u + i
                psa, psb_ = mm(), mm()
                pe_group(psa[0:W], [(wv(kc, i * 128, (i + 1) * 128), hT[kc, hsl[0]:hsl[1]]) for kc in range(8)])
                pe_group(psb_[0:W], [(wv(kc, 256 + i * 128, 256 + (i + 1) * 128), hT[kc, hsl[0]:hsl[1]]) for kc in range(8)])
                sg = tmp[i, 0:W]
                act(sg, psb_[0:W], AF.Sigmoid)
                tt(dst_of(c), psa[0:W], sg, ALU.mult)
                if tail_of is not None:
                    tt(tail_of(c), psa[W - 30:W], tmp[i, W - 30:W], ALU.mult)
            WS.done(key)

    def phaseA(l, T, first_tile, last_tile):
        W = T.W
        vb = l * VL
        load_tile(l, T)
        rmsnorm_hT(W, vb + V_G1)
        for h in range(HEADS):
            k_side(l, T, h, with_q=False)
            v_side(l, T, h)
            for tb, (s, n) in enumerate(T.chunks):
                state_update(T, h, tb, n, first_tile and tb == 0)
        if last_tile:
            ab_units(l, 32, (W - 32, W), lambda c: gt[c])

    def prologue(l, T):
        load_tile(l, T)
        rmsnorm_hT(T.W, l * VL + V_G1)

    def phaseB(l, T, pro_done=False, nxt_tile=None):
        W, off = T.W, T.off
        vb = l * VL
        if not pro_done:
            prologue(l, T)
        act(glu[:, 0:30], halo[:, :], AF.Copy)
        ab_units(l, W, (0, W), lambda c: glu[c, 30:30 + W], lambda c: halo_n[c, :])
        act(halo[:, :], halo_n[:, :], AF.Copy)
        taps = [(c, j) for c in range(8) for j in range(31)]
        nxt = [0]
        pending = []

        def emit_tap():
            i = nxt[0]
            nxt[0] += 1
            c, j = taps[i]
            dg = dslots[i % 32]
            wcol = vcol(vb + V_CW + c * 31 + j)
            if i % 2 == 0:
                act(dg, identb[:], AF.Copy, scale=wcol)
            else:
                ts(dg, identb[:], wcol, None, ALU.mult)

            def fin(c=c, j=j, dg=dg):
                pe_acc(ps_conv[0:W], [(dg, glu[c, j:j + W])], start=(j == 0), stop=(j == 30))
                if j == 30:
                    act(accb[c, 0:W], ps_conv[0:W], AF.Identity, bias=vcol(vb + V_CB + c))
            pending.append(fin)

        def pump(k):
            for _ in range(k):
                if nxt[0] < len(taps):
                    emit_tap()
                if len(pending) > 6 or (nxt[0] >= len(taps) and pending):
                    pending.pop(0)()

        def conv_done():
            return nxt[0] >= len(taps) and not pending

        def conv_ln():
            while not conv_done():
                pump(1)
            for c in range(8):
                act(ycm[c, 0:W], accb[c, 0:W], AF.Copy)
                act(sqb[c, 0:W], accb[c, 0:W], AF.Square)
            psm, psq = mm(), mm()
            pe_group(psm[0:W], [(onesb[:], ycm[c, 0:W]) for c in range(8)])
            pe_group(psq[0:W], [(onesb[:], sqb[c, 0:W]) for c in range(8)])
            mean, var = tmp[2, 0:W], tmp[3, 0:W]
            act(mean, psm[0:W], AF.Copy)
            tt(var, mean, mean, ALU.mult)
            tt(var, psq[0:W], var, ALU.subtract)
            act(var, var, AF.Ln, bias=epsL[:])
            act(var, var, AF.Exp, scale=-0.5)
            for c in range(8):
                tt(accb[c, 0:W], accb[c, 0:W], mean, ALU.subtract)
                tt(accb[c, 0:W], accb[c, 0:W], var, ALU.mult)
                act(ycm[c, 0:W], accb[c, 0:W], AF.Silu, scale=vcol(vb + V_LNG + c), bias=vcol(vb + V_LNB + c))

        wi = w3(w_in, l)
        wc = w3(w_co, l)
        co_done = set()

        def conv_out_unit(u):
            co_done.add(u)
            k1, k2 = ("co", l, u), ("in", l, "gc", u)
            wv1 = WS.get(k1, [(wc[:, :, u * 512:(u + 1) * 512], 0)], 8, 512)
            wv2 = WS.get(k2, [(wi[:, :, OFF_GC + u * 512: OFF_GC + (u + 1) * 512], 0)], 8, 512)
            for i in range(4):
                oc = 4 * u + i
                pco = proj_fm(wv1, i * 128, W, rhs=ycm)
                pgc = proj_fm(wv2, i * 128, W)
                sg = tmp[i % 2, 0:W]
                act(sg, pgc, AF.Sigmoid)
                tt(mA[oc, 0:W], pco, sg, ALU.mult)
            WS.done(k1)
            WS.done(k2)

        nch = len(T.chunks)
        k_side(l, T, 0, 0)
        v_side(l, T, 0, 0)
        for h in range(HEADS):
            bs = h % 2
            qk, kpp, vt = qk_s[bs], kpp_s[bs], vt_s[bs]
            gkey = ("in", l, "g", h)
            gwv = WS.get(gkey, [(wi[:, :, OFF_G + h * 512: OFF_G + (h + 1) * 512], 0)], 8, 512)
            for tb, (s, n) in enumerate(T.chunks):
                pa = V(ps_att.ap[0:n, 0:n], ps_att.key, 0, 512)
                pe_group(pa, [(qk[2 + dc, s:s + n], qk[dc, s:s + n]) for dc in range(2)])
                at = V(attT.ap[0:n, 0:n], attT.key, 0, 256)
                mk = V(maskT.ap[0:n, 0:n], maskT.key, maskT.lo, maskT.hi)
                tt(at, pa, mk, ALU.mult)
                pump(3)
                pob = ps_o2[tb % 2]
                po = pob[:].p(n)
                pe_group(po, [(at, vt[tb].p(n))] + [(qk[dc, s:s + n], Sbf[h * 2 + dc]) for dc in range(2)])
                state_update(T, h, tb, n, False, bs)
                for dc in range(2):
                    act(Sbf[h * 2 + dc], S[h * 2 + dc], AF.Copy)
                pump(3)
                last = (h == HEADS - 1 and nch == 4)
                if nch != 4:
                    ecs = range(4)
                elif last:
                    ecs = {0: [0, 1], 1: [2, 3]}.get(tb, [])
                else:
                    ecs = [tb]
                for ec in ecs:
                    ps = proj_fm(gwv, ec * 128, W)
                    act(gate[ec, 0:W], ps, AF.Silu)
                if last and tb == 1:
                    WS.done(gkey)
                if h + 1 < HEADS:
                    if nch != 4:
                        k_side(l, T, h + 1, bs ^ 1, pump)
                        v_side(l, T, h + 1, bs ^ 1)
                    elif tb == 0:
                        kq_proj(l, T, h + 1, bs ^ 1, pump)
                    elif tb == 1:
                        v_side(l, T, h + 1, bs ^ 1)
                        rope_k(T, h + 1, bs ^ 1, pump)
                    elif tb == 2:
                        rope_q(T, h + 1, bs ^ 1, pump)
                if last and tb in (0, 2):
                    conv_out_unit(tb // 2)
                P.add("dve", lambda e, n=n, po=po: e.bn_stats(out=st6[:].ap[0:n], in_=po.ap), reads=[po], writes=[st6[:]])
                P.add("dve", lambda e, n=n: e.bn_aggr(out=mv[:].ap[0:n], in_=st6[:].ap[0:n]), reads=[st6[:]], writes=[mv[:]])
                act(rs[:].p(n), mv[1:2].p(n), AF.Ln, bias=epsL[:].p(n))
                act(rs[:].p(n), rs[:].p(n), AF.Exp, scale=-0.5)
                pump(2)
                stt(nmr[:].p(n), mv[0:1].p(n), -1.0, rs[:].p(n), ALU.mult, ALU.mult)
                ts(on[tb].p(n), po, rs[:].p(n), nmr[:].p(n), ALU.mult, ALU.add)
                pump(4)
            if not (h == HEADS - 1 and nch == 4):
                WS.done(gkey)
            for ec in range(4):
                pe_transposes([(V(ps7.ap[:, 512 + s:512 + s + n], ps7.key, (512 + s) * 2, (512 + s + n) * 2),
                                on[tb, ec * 128:(ec + 1) * 128].p(n),
                                V(identb.ap[0:n, 0:n], identb.key, 0, 256)) for tb, (s, n) in enumerate(T.chunks)])
                stt(retin[h * 4 + ec, 0:W], ps7[512:512 + W], vcol(vb + V_GNG + h * 4 + ec), gate[ec, 0:W], ALU.mult, ALU.mult)
                pump(3)
            if h == 2:
                conv_ln()
        for u in range(2):
            if u not in co_done:
                conv_out_unit(u)
        wr = w3(w_ro, l)
        for u in range(4):
            kr = ("ro", l, u)
            wvr = WS.get(kr, [(wr[:, :, u * 256:(u + 1) * 256], 0)], 16, 256)
            if u % 2 == 0:
                kg = ("in", l, "gr", u // 2)
                wvg = WS.get(kg, [(wi[:, :, OFF_GR + (u // 2) * 512: OFF_GR + (u // 2 + 1) * 512], 0)], 8, 512)
            for i in range(2):
                oc = 2 * u + i
                pro = proj_fm(wvr, i * 128, W, nk=16, rhs=retin)
                pgr = proj_fm(wvg, (oc % 4) * 128, W)
                sg = tmp[i, 0:W]
                act(sg, pgr, AF.Sigmoid)
                t2 = tmp[2 + i, 0:W]
                tt(t2, pro, sg, ALU.mult)
                tt(ycm[oc, 0:W], t2, mA[oc, 0:W], ALU.add)
            WS.done(kr)
            if u % 2 == 1:
                WS.done(kg)
        wo = w3(w_o, l)
        for u in range(2):
            key = ("wo", l, u)
            wv = WS.get(key, [(wo[:, :, u * 512:(u + 1) * 512], 0)], 8, 512)
            for i in range(4):
                oc = 4 * u + i
                ps = proj_fm(wv, i * 128, W, rhs=ycm)
                tt(xo[oc, 0:W], xt[oc, 0:W], ps, ALU.add)
            WS.done(key)
        rmsnorm_hT(W, vb + V_G2, src=xo)
        w1 = w3(w_m1, l)
        for u in range(8):
            key = ("m1", l, u)
            wv = WS.get(key, [(w1[:, :, u * 512:(u + 1) * 512], 0)], 8, 512)
            for i in range(4):
                fc = 4 * u + i
                ps = proj_fm(wv, i * 128, W)
                r = tmp[1 + (i % 2), 0:W]
                act(r, ps, AF.Relu)
                act(ubuf[fc, 0:W], r, AF.Square)
            WS.done(key)
        if nxt_tile is not None:
            prologue(*nxt_tile)
        w2 = w3(w_m2, l)
        for cp in range(4):
            pss = [mm(), mm()]
            for hk in range(2):
                key = ("m2", l, cp, hk)
                wv = WS.get(key, [(w2[:, hk * 16:(hk + 1) * 16, cp * 256:(cp + 1) * 256], 0)], 16, 256)
                for i in range(2):
                    pe_acc(pss[i][0:W], [(wv(kc, i * 128, (i + 1) * 128), ubuf[hk * 16 + kc, 0:W]) for kc in range(16)],
                           start=(hk == 0), stop=(hk == 1))
                WS.done(key)
            for i in range(2):
                oc = 2 * cp + i
                tt(xo[oc, 0:W], xo[oc, 0:W], pss[i][0:W], ALU.add)
        if l == 0:
            dma("sp", [fm(x1_d)[:, :, off:off + W]], [xo[:, 0:W].ap], ("x1w", (off // TW) % 4), reads=[xo[:, 0:W]],
                writes=[DR("x1", off, off + W)])
        else:
            rstd = tmp[0, 0:W]
            rmsnorm_stats(xo, W, rstd)
            for c in range(8):
                stt(accb[c, 0:W], xo[c, 0:W], vcol(V_FG + c), rstd, ALU.mult, ALU.mult)
            dma("sp", [fm(y_d)[:, :, off:off + W]], [accb[:, 0:W].ap], "out", reads=[accb[:, 0:W]], writes=[DR("y", off, off + W)])

    def s_dram(ap_l):
        return ap_l.rearrange("h (dc p) e -> p (h dc) e", p=128)

    ptiles = [Tile_(t * TW, TW, [(i * 128, 128) for i in range(4)], False) for t in range(NPT)]
    stile = Tile_(SEG, SW, [(0, SW)], True)

    order = []
    for l in range(2):
        order.append((l, stile))
        order += [(l, T) for T in ptiles]
    pro_done = False
    for idx, (l, T) in enumerate(order):
        nxt = order[idx + 1] if idx + 1 < len(order) else None
        if T.sample:
            P.epoch += 1
            dma("sp", [S[:].ap, halo[:].ap], [s_dram(strt[l]), fm(stc[l])], "sld", reads=[], writes=[S[:], halo[:]])
            for hd in range(8):
                act(Sbf[hd], S[hd], AF.Copy)
            phaseB(l, T, pro_done, nxt)
            dma("sp", [s_dram(nsr_s[l]), fm(ncs_s[l])], [S[:].ap, halo[:].ap], "out", reads=[S[:], halo[:]], writes=[DR(("os", l))])
            P.add("dve", lambda e: e.memset(S[:].ap, 0.0), reads=[], writes=[S[:]])
            P.add("dve", lambda e: e.memset(Sbf[:].ap, 0.0), reads=[], writes=[Sbf[:]])
            P.add("dve", lambda e: e.memset(halo[:].ap, 0.0), reads=[], writes=[halo[:]])
        else:
            t = T.off // TW
            if t % 4 == 0:
                P.epoch += 1
            phaseB(l, T, pro_done, nxt)
            if t == NPT - 1:
                dma("sp", [s_dram(nsr_p[l]), fm(ncs_p[l])], [S[:].ap, halo[:].ap], "out", reads=[S[:], halo[:]], writes=[DR(("op", l))])
        pro_done = nxt is not None
    return P, WS


def emit_program(nc, P):
    ops = P.ops
    eng_ops = {e: [] for e in ("pe", "act", "dve", "pool", "sp")}
    pos = {}
    for i, o in enumerate(ops):
        pos[i] = len(eng_ops[o["eng"]])
        eng_ops[o["eng"]].append(i)
    need = set()
    for i, o in enumerate(ops):
        for d in o["deps"]:
            a = ops[d]
            if a["kind"] != "c":
                continue
            if a["eng"] != o["eng"]:
                need.add(d)
            elif a["eng"] != "pe":
                need.add(d)
    sigidx = {}
    for e, lst in eng_ops.items():
        kk = {}
        for i in lst:
            if i in need:
                ep = ops[i]["ep"]
                kk[ep] = kk.get(ep, 0) + 1
                sigidx[i] = kk[ep]
    sems = {}

    def sem(key):
        if key not in sems:
            sems[key] = nc.alloc_semaphore("s_" + "_".join(str(x) for x in (key if isinstance(key, tuple) else (key,))))
        return sems[key]
    final_waits = {}
    for o in ops:
        if o["kind"] != "c":
            final_waits[o["sem"]] = o["cum"]

    def run_engine(ename, e):
        waited = {}
        for i in eng_ops[ename]:
            o = ops[i]
            req = {}
            for d in o["deps"]:
                a = ops[d]
                if a["kind"] == "c":
                    if a["eng"] == ename:
                        if ename == "pe":
                            continue
                    k, v = ("eng", a["eng"], a["ep"]), sigidx[d]
                else:
                    k, v = a["sem"], a["cum"]
                if v > req.get(k, 0):
                    req[k] = v
            for k, v in req.items():
                if v > waited.get(k, 0):
                    e.wait_ge(sem(k), v)
                    waited[k] = v
            r = o["emit"](e)
            if o["kind"] == "c":
                if i in need:
                    r.then_inc(sem(("eng", ename, o["ep"])), 1)
            elif o["kind"] == "d":
                for ins in r:
                    ins.then_inc(sem(o["sem"]), 16)
            else:
                r.then_inc(sem(o["sem"]))
        if ename == "sp":
            for k, v in final_waits.items():
                if v > waited.get(k, 0):
                    e.wait_ge(sem(k), v)

    with nc.Block() as block:
        @block.tensor
        def _(e):
            run_engine("pe", e)

        @block.scalar
        def _(e):
            run_engine("act", e)

        @block.vector
        def _(e):
            run_engine("dve", e)

        @block.gpsimd
        def _(e):
            run_engine("pool", e)

        @block.sync
        def _(e):
            run_engine("sp", e)


def build_nc():
    nc = bass.Bass("TRN2", target_bir_lowering=False)
    cache = {}
    _, ws0 = build_program(nc, True, None, cache)
    P, WS = build_program(nc, False, ws0.rec, cache)
    assert WS.i == len(ws0.rec) and WS.issued == len(ws0.rec)
    emit_program(nc, P)
    return nc


def _host_tables(core):
    pos = np.concatenate([np.arange(SEG), PAST + np.arange(SW)]).astype(np.float32)
    inv = (10000.0 ** (-np.arange(0, 256, 2, dtype=np.float32) / np.float32(256))).astype(np.float32)
    ang = (pos[None, :] * inv[:, None]).astype(np.float32)
    tabs = np.zeros((128, NTAB), np.float32)
    tabs[:, T_COS:T_COS + NTOK] = np.cos(ang)
    tabs[:, T_SIN:T_SIN + NTOK] = np.sin(ang)
    n = np.arange(128, dtype=np.float64)
    for h in range(HEADS):
        g = GAMMA[h]
        tabs[:, T_C + C_DQ + h * 128: T_C + C_DQ + (h + 1) * 128] = (g ** (n + 1.0))[None, :]
        tabs[:, T_C + C_DK + h * 128: T_C + C_DK + (h + 1) * 128] = (g ** (-(n + 1.0)) / 16.0)[None, :]
    m = np.arange(128)
    tabs[:, T_C + C_MASK: T_C + C_MASK + 128] = (m[None, :] >= m[:, None]).astype(np.float32)
    tabs[:, T_C + C_ID: T_C + C_ID + 128] = np.eye(128, dtype=np.float32)
    coef = np.zeros((128, 20), np.float32)
    return tabs, coef


def _pcol(v):
    return np.ascontiguousarray(np.asarray(v, np.float32).reshape(-1, 128).T)


_NC_CACHE = {}


def kernel(x_prompt, x_sample, state_conv, state_ret, norm1_g, w_in, conv_w, conv_b, conv_ln_g, conv_ln_b,
           w_conv_out, ret_gn_g, w_ret_out, w_out, norm2_g, w_mlp1, w_mlp2, final_g):
    f = lambda a: np.ascontiguousarray(np.asarray(a, dtype=np.float32))
    x_prompt, x_sample, state_conv, state_ret = f(x_prompt), f(x_sample), f(state_conv), f(state_ret)
    vecs = np.zeros((128, NV), np.float32)
    for l in range(2):
        b = l * VL
        vecs[:, b + V_G1:b + V_G1 + 8] = _pcol(norm1_g[l])
        vecs[:, b + V_G2:b + V_G2 + 8] = _pcol(norm2_g[l])
        vecs[:, b + V_CB:b + V_CB + 8] = _pcol(conv_b[l])
        vecs[:, b + V_LNG:b + V_LNG + 8] = _pcol(conv_ln_g[l])
        vecs[:, b + V_LNB:b + V_LNB + 8] = _pcol(conv_ln_b[l])
        vecs[:, b + V_GNG:b + V_GNG + 16] = _pcol(ret_gn_g[l])
        cw = np.asarray(conv_w[l], np.float32)
        vecs[:, b + V_CW:b + V_CW + 248] = cw.T.reshape(8, 128, 31).transpose(1, 0, 2).reshape(128, 248)
    vecs[:, V_FG:V_FG + 8] = _pcol(final_g)
    shared = dict(w_in=f(w_in), w_co=f(w_conv_out), w_ro=f(w_ret_out), w_o=f(w_out), w_m1=f(w_mlp1), w_m2=f(w_mlp2), vecs=vecs)
    in_maps = []
    for c in range(NCORES):
        xin = np.concatenate([x_prompt[c % 2].T, x_sample[c].T], axis=1)
        tabs, coef = _host_tables(c)
        m = dict(shared)
        m.update(xin=np.ascontiguousarray(xin), stc=np.ascontiguousarray(state_conv[:, c].transpose(0, 2, 1)),
                 strt=np.ascontiguousarray(state_ret[:, c]), tabs=tabs, coef=coef)
        in_maps.append(m)
    if "nc" not in _NC_CACHE:
        _NC_CACHE["nc"] = build_nc()
    res = run_bass_kernel_spmd(_NC_CACHE["nc"], in_maps, core_ids=list(range(NCORES)))
    R = res.results
    y_p = np.zeros((2, SEQ, D), np.float32)
    y_s = np.zeros((8, SW, D), np.float32)
    for c in range(NCORES):
        yc = R[c]["y"]
        if c < 2:
            y_p[c] = yc[:, :SEG].T
        y_s[c] = yc[:, SEG:].T
    ncs_p = np.stack([R[b]["ncs_p"].transpose(0, 2, 1) for b in range(2)], axis=1)
    nsr_p = np.stack([R[b]["nsr_p"] for b in range(2)], axis=1)
    ncs_s = np.stack([R[c]["ncs_s"].transpose(0, 2, 1) for c in range(NCORES)], axis=1)
    nsr_s = np.stack([R[c]["nsr_s"] for c in range(NCORES)], axis=1)
    return (y_p, y_s, np.ascontiguousarray(ncs_p), np.ascontiguousarray(nsr_p),
            np.ascontiguousarray(ncs_s), np.ascontiguousarray(nsr_s))
```
